# Optimizing a Trainium2 kernel written in Bass

```python
import math
import jax, jax.numpy as jnp
from jax import lax
import numpy as np

D_MODEL = 1024
BATCH = 4
SEQ = 8192
DEPTH = 1

D_MIX = D_MODEL
GLA_WIDTH = D_MIX // 2
GLA_HEADS = 4
GLA_DK = GLA_WIDTH // 2 // GLA_HEADS
GLA_DV = GLA_WIDTH // GLA_HEADS
GLA_RANK = 16
GLA_TAU = 16.0
GLA_CHUNK = 64
DSA_WIDTH = D_MIX - GLA_WIDTH
DSA_HEADS = 8
DSA_DH = DSA_WIDTH // DSA_HEADS
DSA_PATTERN = ((128, 1), (512, 4), (2048, 16))
DSA_BLOCK = 128
REL_BUCKETS = 32
REL_MAX_DIST = 2048
D_FF = 4 * D_MODEL
EPS = 1e-6
NEG = -1e30

IN_SPLITS = (
    GLA_HEADS * GLA_DK,
    GLA_HEADS * GLA_DK,
    GLA_WIDTH,
    GLA_WIDTH,
    GLA_RANK,
    DSA_WIDTH,
    DSA_WIDTH,
    DSA_WIDTH,
)
D_IN = sum(IN_SPLITS)

kernel_name = "hymba_gla_dilated_swa_block"


def rmsnorm(x, g):
    xf = x.astype(jnp.float32)
    y = xf * lax.rsqrt(jnp.mean(xf * xf, axis=-1, keepdims=True) + EPS)
    return (y * g.astype(jnp.float32)).astype(x.dtype)


def t5_bucket(dist):
    max_exact = REL_BUCKETS // 2
    n = np.maximum(dist, 0)
    large = max_exact + (np.log(np.maximum(n, 1) / max_exact)
                         / math.log(REL_MAX_DIST / max_exact)
                         * (REL_BUCKETS - max_exact)).astype(np.int32)
    large = np.minimum(large, REL_BUCKETS - 1)
    return np.where(n < max_exact, n, large).astype(np.int32)


def gla_mixer(q, k, v, glog):
    B, S, H, dk = q.shape
    dv = v.shape[-1]
    C = GLA_CHUNK
    n = S // C
    f32 = jnp.float32
    q = (q.astype(f32) * dk ** -0.5).reshape(B, n, C, H, dk)
    k = k.astype(f32).reshape(B, n, C, H, dk)
    v = v.astype(f32).reshape(B, n, C, H, dv)
    b = jnp.cumsum(glog.astype(f32).reshape(B, n, C, H, dk), axis=2)
    b_last = b[:, :, -1]
    q_dec = q * jnp.exp(b)
    k_inv = k * jnp.exp(-b)
    k_end = k * jnp.exp(b_last[:, :, None] - b)
    causal = jnp.tril(jnp.ones((C, C), dtype=bool))
    att = jnp.einsum('bnihk,bnjhk->bnhij', q_dec, k_inv)
    att = jnp.where(causal, att, 0.0)
    o_intra = jnp.einsum('bnhij,bnjhv->bnihv', att, v)
    inc = jnp.einsum('bnjhk,bnjhv->bnhkv', k_end, v)

    def step(state, inp):
        decay, upd = inp
        return decay[..., None] * state + upd, state

    _, s_prev = lax.scan(step, jnp.zeros((B, H, dk, dv), f32),
                         (jnp.exp(b_last).swapaxes(0, 1), inc.swapaxes(0, 1)))
    s_prev = s_prev.swapaxes(0, 1)
    o_inter = jnp.einsum('bnihk,bnhkv->bnihv', q_dec, s_prev)
    return (o_intra + o_inter).reshape(B, S, H, dv)


def dilated_branch(q, k, v, rel_bias, window, dilation):
    B, S, H, Dh = q.shape
    L = DSA_BLOCK
    span = window // dilation
    unit = dilation * L
    Sp = -(-S // unit) * unit
    n = Sp // dilation
    nb = n // L
    f32 = jnp.float32

    def to_sub(t):
        t = jnp.pad(t, ((0, 0), (0, Sp - S), (0, 0), (0, 0)))
        return t.reshape(B, n, dilation, H, Dh).transpose(0, 2, 3, 1, 4)

    def band(t):
        tb = jnp.pad(t, ((0, 0), (0, 0), (0, 0), (L, 0), (0, 0))).reshape(B, dilation, H, nb + 1, L, Dh)
        return jnp.concatenate([tb[:, :, :, :-1], tb[:, :, :, 1:]], axis=4)

    qb = to_sub(q).reshape(B, dilation, H, nb, L, Dh)
    kb = band(to_sub(k))
    vb = band(to_sub(v))

    steps = L + np.arange(L)[:, None] - np.arange(2 * L)[None, :]
    in_band = (steps >= 0) & (steps <= span)
    key_idx = np.arange(nb)[:, None, None] * L + np.arange(2 * L)[None, None, :] - L
    mask = jnp.asarray(in_band[None] & (key_idx >= 0))
    bias = jnp.transpose(rel_bias[t5_bucket(steps * dilation)], (2, 0, 1)).astype(f32)

    s = jnp.einsum('bdhcqe,bdhcke->bdhcqk', qb, kb).astype(f32) + bias[:, None]
    s = jnp.where(mask, s, NEG)
    m = jnp.max(s, axis=-1, keepdims=True)
    p = jnp.exp(s - m)
    den = jnp.sum(p, axis=-1, keepdims=True)
    o = jnp.einsum('bdhcqk,bdhcke->bdhcqe', p, vb.astype(f32)) / den
    lse = (m + jnp.log(den))[..., 0]
    o = o.reshape(B, dilation, H, n, Dh).transpose(0, 3, 1, 2, 4).reshape(B, Sp, H, Dh)[:, :S]
    lse = lse.reshape(B, dilation, H, n).transpose(0, 3, 1, 2).reshape(B, Sp, H)[:, :S]
    return o, lse


def dilated_mixer(q, k, v, rel_bias):
    q = q * DSA_DH ** -0.5
    outs, lses = [], []
    for window, dilation in DSA_PATTERN:
        o, lse = dilated_branch(q, k, v, rel_bias, window, dilation)
        outs.append(o)
        lses.append(lse)
    w = jax.nn.softmax(jnp.stack(lses, axis=0), axis=0)
    return jnp.sum(w[..., None] * jnp.stack(outs, axis=0), axis=0)


def setup_inputs(seed: int = 0) -> dict:
    key = jax.random.key(seed)
    ks = jax.random.split(key, 16)
    f32 = jnp.float32
    nrm = lambda k, shape, scale: jax.random.normal(k, shape, f32) * scale
    return {
        "x": jax.random.normal(ks[0], (BATCH, SEQ, D_MODEL), f32),
        "attn_norm_g": 1.0 + nrm(ks[1], (DEPTH, D_MODEL), 0.02),
        "w_in": nrm(ks[2], (DEPTH, D_MODEL, D_IN), D_MODEL ** -0.5),
        "gla_gate_w2": nrm(ks[3], (DEPTH, GLA_RANK, GLA_HEADS * GLA_DK), GLA_RANK ** -0.5),
        "gla_gate_b": nrm(ks[4], (DEPTH, GLA_HEADS * GLA_DK), 0.1),
        "gla_norm_g": 1.0 + nrm(ks[5], (DEPTH, GLA_WIDTH), 0.02),
        "rel_bias": nrm(ks[6], (REL_BUCKETS, DSA_HEADS), 0.1),
        "w_out": nrm(ks[7], (DEPTH, D_MIX, D_MODEL), D_MIX ** -0.5),
        "mlp_norm_g": 1.0 + nrm(ks[8], (DEPTH, D_MODEL), 0.02),
        "w_ff1": nrm(ks[9], (DEPTH, D_MODEL, D_FF), D_MODEL ** -0.5),
        "w_ff2": nrm(ks[10], (DEPTH, D_FF, D_MODEL), D_FF ** -0.5),
        "final_norm_g": 1.0 + nrm(ks[11], (D_MODEL,), 0.02),
    }


def reference(x, attn_norm_g, w_in, gla_gate_w2, gla_gate_b, gla_norm_g, rel_bias,
              w_out, mlp_norm_g, w_ff1, w_ff2, final_norm_g):
    B, S, _ = x.shape
    split_at = [int(i) for i in np.cumsum(IN_SPLITS)[:-1]]
    h = x
    for l in range(DEPTH):
        nx = rmsnorm(h, attn_norm_g[l])
        proj = jnp.einsum('bsd,dp->bsp', nx, w_in[l])
        gq, gk, gv, gr, glow, dq, dk_, dv_ = jnp.split(proj, split_at, axis=-1)

        gate_pre = (jnp.einsum('bsr,rk->bsk', glow, gla_gate_w2[l]) + gla_gate_b[l]).astype(jnp.float32)
        glog = jax.nn.log_sigmoid(gate_pre) / GLA_TAU
        o_a = gla_mixer(gq.reshape(B, S, GLA_HEADS, GLA_DK),
                        gk.reshape(B, S, GLA_HEADS, GLA_DK),
                        gv.reshape(B, S, GLA_HEADS, GLA_DV),
                        glog.reshape(B, S, GLA_HEADS, GLA_DK))
        o_a = o_a * lax.rsqrt(jnp.mean(o_a * o_a, axis=-1, keepdims=True) + EPS)
        o_a = o_a * gla_norm_g[l].astype(jnp.float32).reshape(GLA_HEADS, GLA_DV)
        o_a = o_a.reshape(B, S, GLA_WIDTH) * jax.nn.silu(gr.astype(jnp.float32))

        o_b = dilated_mixer(dq.reshape(B, S, DSA_HEADS, DSA_DH),
                            dk_.reshape(B, S, DSA_HEADS, DSA_DH),
                            dv_.reshape(B, S, DSA_HEADS, DSA_DH),
                            rel_bias).reshape(B, S, DSA_WIDTH)

        mixed = jnp.concatenate([o_a, o_b], axis=-1).astype(h.dtype)
        h = h + jnp.einsum('bsm,md->bsd', mixed, w_out[l])

        nm = rmsnorm(h, mlp_norm_g[l])
        a = jnp.square(jax.nn.relu(jnp.einsum('bsd,df->bsf', nm, w_ff1[l])))
        h = h + jnp.einsum('bsf,fd->bsd', a, w_ff2[l])
    return rmsnorm(h, final_norm_g)
```

```python
import math
from contextlib import ExitStack

import numpy as np
import ml_dtypes

import concourse.bass as bass
import concourse.mybir as mybir
from concourse.bass_utils import run_bass_kernel_spmd

F32 = mybir.dt.float32
BF = mybir.dt.bfloat16
ALU = mybir.AluOpType
AF = mybir.ActivationFunctionType

D = 1024
DIN = 3088
DFF = 4096
C_GQ, C_GK, C_GV, C_GR, C_GL, C_DQ, C_DK, C_DV = 0, 256, 512, 1024, 1536, 1552, 2064, 2576
EPS = 1e-6
NEG = -1e30
REG = 2048
RING = 4096
BRANCH_D = (1, 4, 16)


class Sched:
    ENGS = ("pe", "act", "dve", "pool", "sp")

    def __init__(self):
        self.ops = []
        self.lastw = {}
        self.readers = {}
        self.last_real = {}
        self.last_dma = {}
        self.group_total = {"setup"}

    max_ops = None

    def op(self, eng, fn, r=(), w=(), dma=None, extra=()):
        i = len(self.ops)
        if self.max_ops is not None and i >= self.max_ops and fn is not None:
            return None
        w = list(w) + [k for k in r if isinstance(k, str) and k[:2] == "ps" and k[2:3].isdigit()]
        deps = set(extra)
        for k in r:
            j = self.lastw.get(k)
            if j is not None:
                deps.add(j)
        for k in w:
            j = self.lastw.get(k)
            if j is not None:
                deps.add(j)
            for j in self.readers.get(k, ()):
                deps.add(j)
        deps.discard(i)
        for k in w:
            self.lastw[k] = i
            self.readers[k] = []
        for k in r:
            lst = self.readers.setdefault(k, [])
            if dma is None:
                lst[:] = [j for j in lst if not (self.ops[j]["dma"] is None and self.ops[j]["eng"] == eng)]
            lst.append(i)
        self.ops.append(dict(eng=eng, fn=fn, deps=deps, dma=dma, sig=None))
        if fn is not None:
            if dma is not None:
                self.last_dma[dma] = i
            else:
                self.last_real[eng] = i
        return i

    def barrier(self):
        extra = set(self.last_real.values()) | set(self.last_dma.values())
        for e in self.ENGS:
            self.op(e, None, extra=extra)

    def emit(self, nc, es):
        ops = self.ops
        needed = [False] * len(ops)
        for o in ops:
            for j in o["deps"]:
                pj = ops[j]
                if pj["dma"] is None and o["dma"] is None and pj["eng"] == "pe" and o["eng"] == "pe" \
                        and o["fn"] is not None:
                    continue
                needed[j] = True
        cnt = {}
        for i, o in enumerate(ops):
            if o["fn"] is None:
                continue
            if o["dma"] is not None:
                key = "d_" + o["dma"]
                cnt[key] = cnt.get(key, 0) + 16
                o["sig"] = (key, cnt[key])
            elif needed[i]:
                key = "e_" + o["eng"]
                cnt[key] = cnt.get(key, 0) + 1
                o["sig"] = (key, cnt[key])
        sems = {k: es.enter_context(nc.semaphore(k)) for k in cnt}
        self.sem_counts = cnt
        per_eng = {e: [o for o in ops if o["eng"] == e] for e in self.ENGS}

        def run(engname, e):
            waited = {}
            for o in per_eng[engname]:
                want = {}
                for j in o["deps"]:
                    pj = ops[j]
                    if pj["sig"] is None:
                        continue
                    if pj["dma"] is None and o["dma"] is None and pj["eng"] == "pe" and engname == "pe" \
                            and o["fn"] is not None:
                        continue
                    key, val = pj["sig"]
                    if pj["dma"] in self.group_total:
                        val = cnt[key]
                    if val > want.get(key, 0):
                        want[key] = val
                for key, val in want.items():
                    if waited.get(key, 0) >= val:
                        continue
                    e.wait_ge(sems[key], val)
                    waited[key] = val
                if o["fn"] is not None:
                    ins = o["fn"](e)
                    if o["sig"] is not None:
                        ins.then_inc(sems[o["sig"][0]], 16 if o["dma"] is not None else 1)

        with nc.Block() as block:
            @block.tensor
            def _(e):
                run("pe", e)

            @block.scalar
            def _(e):
                run("act", e)

            @block.vector
            def _(e):
                run("dve", e)

            @block.gpsimd
            def _(e):
                run("pool", e)

            @block.sync
            def _(e):
                run("sp", e)


def I(name, *a, **k):
    return lambda e: getattr(e, name)(*a, **k)


class Carver:
    def __init__(self, big, start, limit):
        self.big, self.off, self.limit = big, start, limit

    def get(self, dtype, free, parts=128):
        n = int(np.prod(free))
        nb = n * (4 if dtype is F32 else 2)
        off = self.off
        self.off += (nb + 63) // 64 * 64
        assert self.off <= self.limit, (self.off, self.limit)
        ap = self.big[0:parts, off // 2: off // 2 + nb // 2]
        if dtype is F32:
            ap = ap.bitcast(F32)
        if len(free) == 2:
            ap = ap.rearrange("p (a b) -> p a b", a=free[0])
        elif len(free) == 3:
            ap = ap.rearrange("p (a b c) -> p a b c", a=free[0], b=free[1])
        elif len(free) == 4:
            ap = ap.rearrange("p (a b c d) -> p a b c d", a=free[0], b=free[1], c=free[2])
        return ap


def build(NT, NW=0, stop_after=None):
    first_region_has_prev = False
    NREG = (NT + NW) // REG
    NWR = NW // REG
    nc = bass.Bass("TRN2", target_bir_lowering=False)
    es = ExitStack()
    S = Sched()
    if isinstance(stop_after, int):
        S.max_ops = stop_after

    def din(name, shape, dt=F32):
        return nc.dram_tensor(name, list(shape), dt, kind="ExternalInput").ap()

    x_d = din("x", [NT + NW, D])
    halo_d = din("halo", [128, 1])
    win_d = din("w_in", [D, DIN])
    gA_d = din("gAb", [128, D])
    w2e_d = din("w2e", [17, 256])
    gng_d = din("gng", [128, 512])
    relb_d = din("relb", [33, 8])
    oh_d = din("oh", [33, 3 * 383])
    wout_d = din("w_out", [D, D])
    gMb_d = din("gMb", [128, D])
    w1_d = din("w_ff1", [D, DFF])
    w2_d = din("w_ff2", [DFF, D])
    gF_d = din("gF", [128, D])
    ident_d = din("ident", [128, 128], BF)
    jflip_d = din("jflip", [128, 128])
    lincl_d = din("lincl", [128, 128])
    lst_d = din("lst", [128, 128])
    maskT_d = din("maskT", [128, 128], BF)
    swp_d = din("swp", [128, 256])
    out_d = nc.dram_tensor("out", [NT, D], F32, kind="ExternalOutput").ap()
    gx_t = nc.dram_tensor("gx_scr", [8, 3 * 383], F32, kind="Internal")
    gx_d = gx_t.ap()
    bias_d = nc.dram_tensor("bias_scr", [128, 3 * 8 * 256], BF, kind="Internal").ap()
    mx_d = nc.dram_tensor("mx_scr", [NT // 512, 128, 8, 512], BF, kind="Internal").ap()

    NBF = 106000
    big = es.enter_context(nc.sbuf_tensor("big", [128, NBF], BF))
    ps = [es.enter_context(nc.psum_tensor(f"ps{i}", [128, 512], F32)) for i in range(8)]

    def psf(i):
        return ps[i][:, :]

    def psb(i):
        return ps[i][:, :].bitcast(BF)

    CM = Carver(big, 0, NBF * 2)
    ident = CM.get(BF, [128])
    xs = [CM.get(F32, [1024]) for _ in range(3)]
    sq_junk = CM.get(BF, [128])
    epsT = CM.get(F32, [1])
    stat = CM.get(F32, [64])
    COMMON_END = CM.off

    def dma(q, out, in_, r, w, key):
        return S.op(q, I("dma_start", out=out, in_=in_), r=r, w=w, dma=key)

    def cp(eng, out, in_, r, w):
        S.op(eng, I("copy" if eng == "act" else "tensor_copy", out, in_), r=r, w=w)

    dma("sp", ident, ident_d, [], ["ident"], "setup")
    S.op("pool", I("memset", epsT, EPS), w=["epsT"])

    P1 = Carver(big, COMMON_END, NBF * 2)
    Win = P1.get(BF, [8, DIN])
    gA = P1.get(F32, [D])
    w2e = P1.get(F32, [256], parts=17)
    gng = P1.get(F32, [512])
    lincl = P1.get(F32, [128])
    lst = P1.get(F32, [128])
    maskT = P1.get(BF, [128])
    swp = P1.get(F32, [2, 128])
    negM = P1.get(F32, [8])
    haloM = P1.get(F32, [1])
    qT = P1.get(BF, [4, REG])
    kT = P1.get(BF, [4, RING])
    vT = P1.get(BF, [4, RING])
    Sf = P1.get(F32, [2, 128])
    Sb = P1.get(BF, [2, 128])
    STAGE0 = P1.off
    PA = Carver(big, STAGE0, NBF * 2)
    xn = [PA.get(BF, [1024]) for _ in range(2)]
    xnT = PA.get(BF, [8, 512])
    mixA = [PA.get(BF, [4, 512]) for _ in range(1)]
    gqT_sb = PA.get(BF, [2, 512])
    gkT_sb = PA.get(BF, [2, 512])
    glow = [PA.get(F32, [512], parts=17) for _ in range(1)]
    G = []
    for _ in range(2):
        G.append(dict(
            gk_sb=PA.get(F32, [256]), gr_sb=PA.get(F32, [512]), v_bf=PA.get(BF, [512]),
            lsp=PA.get(F32, [256]), EbT=PA.get(F32, [2, 128]), EnbT=PA.get(F32, [2, 128]), Erem=PA.get(F32, [256]),
            qdp=PA.get(BF, [2, 2, 128]), kiT=PA.get(BF, [2, 128]), kend=PA.get(BF, [256]), attm=PA.get(BF, [4, 128]),
            ep_e=PA.get(F32, [512]), ep_on=PA.get(F32, [512]), mix_tm=PA.get(BF, [512])))
    PB = Carver(big, STAGE0, NBF * 2)
    ACC = PB.get(F32, [2, REG])
    mixD = PB.get(BF, [REG])
    PT = [PB.get(BF, [2, 2, 128]) for _ in range(4)]
    Vb = [PB.get(BF, [2, 128]) for _ in range(8)]
    qpad = PB.get(BF, [2, REG])
    recb = PB.get(F32, [512])
    biasT = PB.get(BF, [3, 8, 256])
    P0 = Carver(big, STAGE0, NBF * 2)
    Hst = [P0.get(F32, [256]) for _ in range(2)]
    relb = P0.get(F32, [8], parts=33)
    oh = P0.get(F32, [3 * 383], parts=33)
    gxs = P0.get(F32, [3 * 383], parts=8)
    jflip = P0.get(F32, [128])
    print("SBUF bytes: common", COMMON_END, "persist", STAGE0, "A", PA.off, "B", PB.off, "setup", P0.off)

    for dst, src, k in ((haloM, halo_d, "haloM"), (gA, gA_d, "gA"), (w2e, w2e_d, "w2e"), (gng, gng_d, "gng"), (jflip, jflip_d, "jflip"),
                        (lincl, lincl_d, "lincl"), (lst, lst_d, "lst"), (maskT, maskT_d, "maskT"),
                        (swp, swp_d.rearrange("p (a b) -> p a b", a=2), "swp"), (relb, relb_d, "relb"),
                        (oh, oh_d, "oh")):
        dma("sp", dst, src, [], [k], "setup")
    S.op("pool", I("memset", negM, 0.0), w=["negM"])
    S.op("pool", I("memset", Sf, 0.0), w=["Sf"])
    S.op("pool", I("memset", Sb, 0.0), w=["Sb"])

    cast_engs = ("dve", "pool", "act")
    S.group_total.add("setupw")
    for kc in range(8):
        dma("pool", Win[:, kc, :], win_d[kc * 128:(kc + 1) * 128, :], [], [("win", kc)], "setupw")

    for br in range(3):
        S.op("pe", I("matmul", psf(4)[0:8, 0:383], relb, oh[:, br * 383:(br + 1) * 383], start=True, stop=True),
             r=["relb", "oh"], w=["ps4"])
        cp("dve", gxs[:, br * 383:(br + 1) * 383], psf(4)[0:8, 0:383], ["ps4"], ["gxs"])
    dma("sp", gx_d, gxs, ["gxs"], ["gx_d"], "gx")
    bi = 0
    for br in range(3):
        for h in range(8):
            j = bi % 2
            for kh in range(2):
                src = bass.AP(gx_t, h * (3 * 383) + br * 383 + 128 * (1 - kh), [[1, 128], [1, 128]])
                dma("sp", Hst[j][:, kh * 128:(kh + 1) * 128], src, ["gx_d"], [("Hst", j, kh)], f"Hst{j}{kh}")
            pb = 4 + (bi % 2)
            S.op("pe", I("matmul", psf(pb)[:, 0:256], jflip, Hst[j], start=True, stop=True),
                 r=[("Hst", j, 0), ("Hst", j, 1), "jflip"], w=[f"ps{pb}"])
            cp("act" if bi % 2 else "dve", biasT[:, br, h, :], psf(pb)[:, 0:256], [f"ps{pb}"], ["biasT"])
            bi += 1
    dma("sp", bias_d, biasT.rearrange("p a b c -> p (a b c)"), ["biasT"], ["bias_d"], "biasd")
    S.barrier()
    if stop_after == "setup":
        S.emit(nc, es)
        es.close()
        return nc, S, dict(biasT=biasT, Win=Win)

    evac_ctr = [0]

    def evac(out, in_, r, w, scale=None):
        evac_ctr[0] += 1
        if evac_ctr[0] % 2:
            if scale is None:
                S.op("act", I("copy", out, in_), r=r, w=w)
            else:
                S.op("act", I("activation", out, in_, AF.Copy, scale=scale), r=r, w=w)
        else:
            if scale is None:
                S.op("dve", I("tensor_copy", out, in_), r=r, w=w)
            else:
                S.op("dve", I("tensor_scalar", out, in_, scale, None, ALU.mult), r=r, w=w)

    mm_ctr = [0]

    def mm_bank():
        mm_ctr[0] += 1
        return 2 + (mm_ctr[0] % 2)

    xs_ctr = [0]

    def rms_to_T(src, src_keys, xn_t, xn_key, dstT, dst_key, sub, st_off, gbc=None, bank=0, defer=False):
        ssq = stat[:, st_off:st_off + 1]
        rstd = stat[:, st_off + 1:st_off + 2]
        bk = f"ps{bank}"
        S.op("act", I("activation", xn_t, src, AF.Square, accum_out=ssq),
             r=src_keys, w=[xn_key, ("stat", st_off)])
        S.op("act", I("activation", rstd, ssq, AF.Ln, scale=1.0 / D, bias=epsT[:, 0:1]),
             r=[("stat", st_off)], w=[("stat", st_off + 1)])
        S.op("act", I("activation", rstd, rstd, AF.Exp, scale=-0.5), r=[("stat", st_off + 1)], w=[("stat", st_off + 1)])
        if gbc is None:
            S.op("act", I("activation", xn_t, src, AF.Copy, scale=rstd),
                 r=src_keys + [("stat", st_off + 1)], w=[xn_key])
        else:
            S.op("dve", I("scalar_tensor_tensor", xn_t, src, rstd, gbc, ALU.mult, ALU.mult),
                 r=src_keys + [("stat", st_off + 1), "gM"], w=[xn_key])
        def part_b():
            for kc in range(8):
                S.op("pe", I("transpose", psb(bank)[:, kc * 128:(kc + 1) * 128], xn_t[:, kc * 128:(kc + 1) * 128], ident),
                     r=[xn_key], w=[bk])
            S.op("dve", I("tensor_copy", dstT[:, :, sub * 128:(sub + 1) * 128], psb(bank).rearrange("p (a b) -> p a b", a=8)),
                 r=[bk], w=[dst_key])
        if defer:
            return part_b
        part_b()

    def projA(c0, M, out, wkey, scale=None):
        b = mm_bank()
        for kc in range(8):
            S.op("pe", I("matmul", psf(b)[0:M, :], Win[:, kc, c0:c0 + M], xnT[:, kc, :], start=(kc == 0), stop=(kc == 7)),
                 r=[("xnT", 0), ("xnT", 1), ("xnT", 2), ("xnT", 3)], w=[f"ps{b}"])
        evac(out, psf(b)[0:M, :], [f"ps{b}"], [wkey], scale=scale)

    def projB(sub, c0, N, b):
        for kc in range(8):
            S.op("pe", I("matmul", psf(b)[:, 0:N], xnT[:, kc, sub * 128:(sub + 1) * 128], Win[:, kc, c0:c0 + N],
                         start=(kc == 0), stop=(kc == 7)),
                 r=[("xnT", sub)], w=[f"ps{b}"])

    P2 = Carver(big, COMMON_END, NBF * 2)
    W1 = P2.get(BF, [8, DFF])
    W2 = P2.get(BF, [32, D])
    Wo = P2.get(BF, [8, D])
    gM = P2.get(F32, [D])
    gF = P2.get(F32, [D])
    TT = 256
    NS = TT // 128
    mxT = P2.get(BF, [8, TT])
    hT = [P2.get(F32, [D]) for _ in range(NS)]
    xn2 = [P2.get(BF, [D]) for _ in range(2)]
    hnT = P2.get(BF, [8, TT])
    aT = P2.get(BF, [32, TT])
    rsb = [P2.get(BF, [TT]) for _ in range(2)]
    print("SBUF bytes phase2:", P2.off)

    S.group_total.add("setup2p")
    S.group_total.add("w1early")
    for R in range(NREG):
        main = R >= NWR
        warm_kv = (not main) and R == NWR - 1
        for g in glow:
            S.op("pool", I("memset", g, 1.0), w=[("glow", id(g))])
        for gi in range(2):
            S.op("pool", I("memset", G[gi]["qdp"], 0.0), w=[("qdT", gi)])

        def rms_part(T, sub, defer=False):
            tok0 = R * REG + T * 512
            j = xs_ctr[0] % 3
            xs_ctr[0] += 1
            t0 = tok0 + sub * 128
            dma("sp", xs[j], x_d[t0:t0 + 128, :], [], [("xs", j)], f"xs{j}")
            return rms_to_T(xs[j], [("xs", j)], xn[sub % 2], ("xn", sub % 2), xnT, ("xnT", sub), sub, 2 * (sub % 2),
                            gbc=gA, bank=sub % 2, defer=defer)

        def proj_part(T):
            tok0 = R * REG + T * 512
            rt0 = T * 512
            ring0 = tok0 % RING
            if main:
                for c in range(2):
                    projA(C_GQ + c * 128, 128, gqT_sb[:, c, :], ("gqT", c))
                for c in range(2):
                    projA(C_GK + c * 128, 128, gkT_sb[:, c, :], ("gkT", c))
            gl = glow[0]
            projA(C_GL, 16, gl[0:16, :], ("glow", id(gl)))
            if main:
                for c in range(4):
                    projA(C_DQ + c * 128, 128, qT[:, c, rt0:rt0 + 512], "qT", scale=0.125)
            if main or warm_kv:
                for c in range(4):
                    projA(C_DK + c * 128, 128, kT[:, c, ring0:ring0 + 512], "kT")
                for c in range(4):
                    projA(C_DV + c * 128, 128, vT[:, c, ring0:ring0 + 512], "vT")

        def F1(si):
            T, sub = divmod(si, 4)
            g = G[si % 2]; p = si % 2
            sl = slice(sub * 128, (sub + 1) * 128)
            gl = glow[0]
            glk = ("glow", id(gl))
            S.op("pe", I("matmul", psf(6)[:, 0:256], gl[:, sl], w2e, start=True, stop=True), r=[glk], w=["ps6"])
            S.op("act", I("activation", g["lsp"], psf(6)[:, 0:256], AF.Exp, scale=-1.0), r=["ps6"], w=[("lsp", p)])
            S.op("act", I("activation", g["lsp"], g["lsp"], AF.Ln, bias=1.0), r=[("lsp", p)], w=[("lsp", p)])
            projB(sub, C_GK, 256, 2)
            cp("act", g["gk_sb"], psf(2)[:, 0:256], ["ps2"], [("gk_sb", p)])
            projB(sub, C_GV, 512, 3)
            cp("dve", g["v_bf"], psf(3), ["ps3"], [("v_bf", p)])
            if main:
                projB(sub, C_GR, 512, 2)
                S.op("act", I("activation", g["ep_e"], psf(2), AF.Exp, scale=-1.0), r=["ps2"], w=[("ep_e", p)])
                S.op("dve", I("tensor_tensor", g["gr_sb"], psf(2), gng, ALU.mult), r=["ps2"], w=[("gr_sb", p)])
                S.op("act", I("activation", g["ep_e"], g["ep_e"], AF.Ln, bias=1.0), r=[("ep_e", p)], w=[("ep_e", p)])
                S.op("act", I("activation", g["ep_e"], g["ep_e"], AF.Exp, scale=-1.0), r=[("ep_e", p)], w=[("ep_e", p)])
                S.op("pool", I("tensor_tensor", g["gr_sb"], g["gr_sb"], g["ep_e"], ALU.mult),
                     r=[("gr_sb", p), ("ep_e", p)], w=[("gr_sb", p)])

        def F2(si):
            T, sub = divmod(si, 4)
            g = G[si % 2]; p = si % 2
            cb = 4 if main else 5
            sl = slice(sub * 128, (sub + 1) * 128)
            lsp = g["lsp"]
            for c in range(2):
                S.op("pe", I("matmul", psf(cb)[:, c * 128:(c + 1) * 128], lsp[:, c * 128:(c + 1) * 128], lincl,
                             start=True, stop=True), r=[("lsp", p)], w=[f"ps{cb}"])
            S.op("pe", I("matmul", psf(cb)[:, 256:512], lst, lsp, start=True, stop=True), r=[("lsp", p)], w=[f"ps{cb}"])
            cT = psf(cb)[:, 0:256].rearrange("p (a b) -> p a b", a=2)
            if main:
                S.op("act", I("activation", g["EbT"], cT, AF.Exp, scale=-1.0 / 16), r=[f"ps{cb}"], w=[("EbT", p)])
                S.op("act", I("activation", g["EnbT"], cT, AF.Exp, scale=1.0 / 16), r=[f"ps{cb}"], w=[("EnbT", p)])
            else:
                S.op("act", I("activation", g["EbT"][:, :, 127:128], cT[:, :, 127:128], AF.Exp, scale=-1.0 / 16),
                     r=[f"ps{cb}"], w=[("EbT", p)])
            S.op("act", I("activation", g["Erem"], psf(cb)[:, 256:512], AF.Exp, scale=-1.0 / 16), r=[f"ps{cb}"], w=[("Erem", p)])
            if main:
                for hd in range(2):
                    pr = slice(hd * 64, (hd + 1) * 64)
                    S.op("dve", I("scalar_tensor_tensor", g["qdp"][pr, :, hd, :], gqT_sb[pr, :, sl], 0.125, g["EbT"][pr, :, :],
                                  ALU.mult, ALU.mult), r=[("gqT", 0), ("gqT", 1), ("EbT", p)], w=[("qdT", p)])
                S.op("pool", I("tensor_tensor", g["kiT"], gkT_sb[:, :, sl], g["EnbT"], ALU.mult),
                     r=[("gkT", 0), ("gkT", 1), ("EnbT", p)], w=[("kiT", p)])
            S.op("pool", I("tensor_tensor", g["kend"], g["gk_sb"], g["Erem"], ALU.mult),
                 r=[("gk_sb", p), ("Erem", p)], w=[("kend", p)])

        def B1(si):
            g = G[si % 2]; p = si % 2
            if not main:
                return
            for h in range(4):
                c = h // 2
                S.op("pe", I("matmul", psf(6)[:, h * 128:(h + 1) * 128], g["kiT"][:, c, :], g["qdp"][:, c, h % 2, :],
                             start=True, stop=True), r=[("kiT", p), ("qdT", p)], w=["ps6"])
            S.op("dve", I("tensor_tensor", g["attm"], psf(6).rearrange("p (a b) -> p a b", a=4),
                          maskT.unsqueeze(1).to_broadcast([128, 4, 128]), ALU.mult), r=["ps6"], w=[("attm", p)])

        def B2(si):
            g = G[si % 2]; p = si % 2
            ob = (7, 5)[si % 2]; obk = f"ps{ob}"
            v_bf, kend, EbT = g["v_bf"], g["kend"], g["EbT"]
            for h in (range(4) if main else ()):
                c = h // 2
                S.op("pe", I("matmul", psf(ob)[:, h * 128:(h + 1) * 128], g["attm"][:, h, :], v_bf[:, h * 128:(h + 1) * 128],
                             start=True, stop=False), r=[("attm", p), ("v_bf", p)], w=[obk])
                S.op("pe", I("matmul", psf(ob)[:, h * 128:(h + 1) * 128], g["qdp"][:, c, h % 2, :], Sb[:, c, :],
                             start=False, stop=True), r=[("qdT", p), "Sb"], w=[obk])
            for c in range(2):
                for hd in range(2):
                    h = 2 * c + hd
                    S.op("pe", I("matmul", psf(4)[:, c * 256 + hd * 128:c * 256 + (hd + 1) * 128], kend[:, c * 128:(c + 1) * 128],
                                 v_bf[:, h * 128:(h + 1) * 128], start=True, stop=True),
                         r=[("kend", p), ("v_bf", p)], w=["ps4"])
            for c in range(2):
                for hd in range(2):
                    pr = slice(hd * 64, (hd + 1) * 64)
                    S.op("dve", I("scalar_tensor_tensor", Sf[pr, c, :], Sf[pr, c, :], EbT[pr, c, 127:128],
                                  psf(4)[pr, c * 256 + hd * 128:c * 256 + (hd + 1) * 128], ALU.mult, ALU.add),
                         r=["ps4", ("EbT", p), "Sf"], w=["Sf"])
            cp("dve", Sb, Sf, ["Sf"], ["Sb"])

        def B3(si):
            T, sub = divmod(si, 4)
            g = G[si % 2]; p = si % 2
            ob = (7, 5)[si % 2]; obk = f"ps{ob}"
            sl = slice(sub * 128, (sub + 1) * 128)
            if not main:
                return
            ep_e, ep_on, gr_sb, mix_tm = g["ep_e"], g["ep_on"], g["gr_sb"], g["mix_tm"]
            o4 = psf(ob).rearrange("p (a b) -> p a b", a=4)
            for h in range(4):
                S.op("act", I("activation", sq_junk[:, 0:128], psf(ob)[:, h * 128:(h + 1) * 128], AF.Square,
                              accum_out=stat[:, 8 + h:9 + h]), r=[obk], w=["sq_junk", ("stat", 8)])
            S.op("dve", I("tensor_scalar", stat[:, 12:16], stat[:, 8:12], 1.0 / 128, EPS, ALU.mult, ALU.add),
                 r=[("stat", 8)], w=[("stat", 12)])
            S.op("act", I("activation", stat[:, 12:16], stat[:, 12:16], AF.Ln), r=[("stat", 12)], w=[("stat", 12)])
            S.op("act", I("activation", stat[:, 12:16], stat[:, 12:16], AF.Exp, scale=-0.5),
                 r=[("stat", 12)], w=[("stat", 12)])
            S.op("dve", I("tensor_tensor", ep_on.rearrange("p (a b) -> p a b", a=4), o4,
                          stat[:, 12:16].unsqueeze(2).to_broadcast([128, 4, 128]), ALU.mult),
                 r=[obk, ("stat", 12)], w=[("ep_on", p)])
            S.op("dve", I("tensor_tensor", mix_tm, ep_on, gr_sb, ALU.mult), r=[("ep_on", p), ("gr_sb", p)], w=[("mix_tm", p)])

        def B3b(si):
            T, sub = divmod(si, 4)
            g = G[si % 2]; p = si % 2
            sl = slice(sub * 128, (sub + 1) * 128)
            if not main:
                return
            mix_tm = g["mix_tm"]
            for c in range(4):
                S.op("pe", I("transpose", psb(1)[:, c * 128:(c + 1) * 128], mix_tm[:, c * 128:(c + 1) * 128], ident),
                     r=[("mix_tm", p)], w=["ps1"])
            cp("act", mixA[0][:, :, sl], psb(1)[:, 0:512].rearrange("p (a b) -> p a b", a=4), ["ps1"], [("mixA", 0)])
            if sub == 3:
                tg = (R * REG + T * 512 - NW) // 512
                dma("pool", mx_d[tg, :, 0:4, :], mixA[0], [("mixA", 0)], [("mx", tg, "a")], "mixA0")

        NSUB = REG // 128
        for sub in range(4):
            rms_part(0, sub)
        late = None
        for si in range(NSUB + 2):
            if si < NSUB:
                T, sub = divmod(si, 4)
                if sub == 0 and si >= 2:
                    B3b(si - 2)
                    B1(si - 1)
                if sub == 0:
                    proj_part(T)
                if T + 1 < NSUB // 4:
                    late = rms_part(T + 1, sub, defer=True)
                F1(si)
            if si >= 2 and not (si < NSUB and si % 4 == 0):
                B3b(si - 2)
            if 1 <= si <= NSUB and not (si < NSUB and si % 4 == 0 and si >= 2):
                B1(si - 1)
            if si < NSUB:
                F2(si)
            if 1 <= si <= NSUB:
                B2(si - 1)
                B3(si - 1)
            if late is not None:
                late()
                late = None
        if not main:
            continue
        S.barrier()
        if stop_after == "A":
            S.emit(nc, es)
            es.close()
            return nc, S, {}

        dma("sp", biasT.rearrange("p a b c -> p (a b c)"), bias_d, ["bias_d"], ["biasT"], "biasl")
        for v in Vb:
            S.op("pool", I("memset", v, 1.0), w=[("Vb", id(v))])
        S.op("pool", I("memset", qpad, 0.0), w=["qpad"])
        for hp in range(4):
            for q_ in range(2):
                qs_ = slice(q_ * 1024, (q_ + 1) * 1024)
                S.op("act", I("copy", qpad[0:64, 0, qs_], qT[0:64, hp, qs_]), r=["qT"], w=["qpad"])
                S.op("dve", I("tensor_copy", qpad[64:128, 1, qs_], qT[64:128, hp, qs_]), r=["qT"], w=["qpad"])
            vcache = {}
            vb_ctr = [0]

            def get_vb(d, kstart_ring, ident_key):
                if ident_key in vcache:
                    return vcache[ident_key]
                slot = vb_ctr[0] % len(Vb)
                vb_ctr[0] += 1
                for k_, v_ in list(vcache.items()):
                    if v_ == slot:
                        del vcache[k_]
                vbank = slot % 2
                S.op("pe", I("transpose", psb(vbank)[:, 0:128], vT[:, hp, kstart_ring:kstart_ring + 127 * d + 1:d], ident),
                     r=["vT"], w=[f"ps{vbank}"])
                vflat = Vb[slot].rearrange("p a (b c) -> p (a b) c", b=2)
                cp("dve", vflat[:, 0:4:3, :], psb(vbank)[:, 0:128].rearrange("p (a b) -> p a b", a=2),
                   [f"ps{vbank}"], [("Vb", id(Vb[slot]))])
                vcache[ident_key] = slot
                return slot

            tiles = []
            for br, d in enumerate(BRANCH_D):
                span = 128 * d
                for r_ in range(d):
                    for c in range(REG // span):
                        qrel = c * span + r_
                        qabs = R * REG + qrel
                        has_prev = (qabs - span) >= 0
                        tiles.append(dict(br=br, d=d, span=span, qrel=qrel, qabs=qabs, has_prev=has_prev,
                                          kc_ring=qabs % RING, kp_ring=(qabs - span) % RING,
                                          halves=([0] if has_prev else []) + [1], i=len(tiles),
                                          qs=slice(qrel, qrel + 127 * d + 1, d)))

            def VT(t):
                t["vslots"] = {}
                if t["has_prev"]:
                    t["vslots"][0] = get_vb(t["d"], t["kp_ring"], (t["br"], t["qabs"] - t["span"]))
                t["vslots"][1] = get_vb(t["d"], t["kc_ring"], (t["br"], t["qabs"]))

            def SC(t):
                i, d, br, halves, qs = t["i"], t["d"], t["br"], t["halves"], t["qs"]
                sb_ = 4 + (i % 4)
                pt = PT[i % 4]
                ptk = ("PT", i % 4)
                S4 = psf(sb_).rearrange("p (a b c) -> p a b c", a=2, b=2)
                for hd in range(2):
                    for kh in halves:
                        kst = t["kp_ring"] if kh == 0 else t["kc_ring"]
                        S.op("pe", I("matmul", S4[:, hd, kh, :], kT[:, hp, kst:kst + 127 * d + 1:d],
                                     qpad[:, hd, qs], start=True, stop=False),
                             r=["kT", "qpad"], w=[f"ps{sb_}"])
                        S.op("pe", I("matmul", S4[:, hd, kh, :], ident,
                                     biasT[:, br, 2 * hp + hd, kh * 128:(kh + 1) * 128], start=False, stop=True),
                             r=["biasT"], w=[f"ps{sb_}"])
                k0 = halves[0]
                in_halo = t["has_prev"] and (t["qabs"] - t["span"]) < NW
                nb_ = negM[:, hp:hp + 1]
                if in_halo:
                    S.op("act", I("activation", pt[:, :, 0, :], S4[:, :, 0, :], AF.Exp, bias=haloM[:, 0:1]),
                         r=[f"ps{sb_}"], w=[ptk])
                    S.op("act", I("activation", pt[:, :, 1, :], S4[:, :, 1, :], AF.Exp, bias=nb_),
                         r=[f"ps{sb_}", "negM"], w=[ptk])
                else:
                    S.op("act", I("activation", pt[:, :, k0:2, :], S4[:, :, k0:2, :], AF.Exp, bias=nb_),
                         r=[f"ps{sb_}", "negM"], w=[ptk])

            def PV(t):
                i, halves, vslots, qs = t["i"], t["halves"], t["vslots"], t["qs"]
                pt = PT[i % 4]
                ptk = ("PT", i % 4)
                ud = 2 + (i % 2)
                UD = psf(ud)[:, 0:256].rearrange("p (a b) -> p a b", a=2)
                for hd in range(2):
                    for n_, kh in enumerate(halves):
                        S.op("pe", I("matmul", UD[:, hd, :], Vb[vslots[kh]][:, hd, :], pt[:, hd, kh, :],
                                     start=(n_ == 0), stop=(n_ == len(halves) - 1)),
                             r=[ptk, ("Vb", id(Vb[vslots[kh]]))], w=[f"ps{ud}"])
                if t["br"] == 0:
                    S.op("dve", I("tensor_copy", ACC[:, :, qs], UD), r=[f"ps{ud}"], w=["ACC"])
                else:
                    S.op("dve", I("tensor_tensor", ACC[:, :, qs], ACC[:, :, qs], UD, ALU.add),
                         r=[f"ps{ud}", "ACC"], w=["ACC"])

            nt_ = len(tiles)
            for i in range(nt_ + 2):
                if i < nt_:
                    VT(tiles[i])
                if 1 <= i <= nt_:
                    SC(tiles[i - 1])
                if i >= 2:
                    PV(tiles[i - 2])
            for sl4 in range(4):
                ts_ = slice(sl4 * 512, (sl4 + 1) * 512)
                S.op("pe", I("matmul", psf(1), swp[:, 0, :], ACC[:, 0, ts_], start=True, stop=False), r=["ACC"], w=["ps1"])
                S.op("pe", I("matmul", psf(1), swp[:, 1, :], ACC[:, 1, ts_], start=False, stop=True), r=["ACC"], w=["ps1"])
                S.op("act", I("activation", recb, psf(1), AF.Ln), r=["ps1"], w=["recb"])
                S.op("act", I("activation", recb, recb, AF.Exp, scale=-1.0), r=["recb"], w=["recb"])
                for hd in range(2):
                    pr = slice(hd * 64, (hd + 1) * 64)
                    S.op("dve", I("tensor_tensor", mixD[pr, ts_], ACC[pr, hd, ts_], recb[pr, :], ALU.mult),
                         r=["ACC", "recb"], w=["mixD"])
            for T in range(4):
                tg = (R * REG - NW) // 512 + T
                dma("pool", mx_d[tg, :, 4 + hp, :], mixD[:, T * 512:(T + 1) * 512], ["mixD"], [("mx", tg, hp)], "mixD")
        S.barrier()

    if stop_after == "B":
        S.emit(nc, es)
        es.close()
        return nc, S, {}
    for g_ in ("setup2", "wog", "w1g0", "w1g1", "w2g"):
        S.group_total.add(g_)
    dma("sp", gM, gMb_d, [], ["gM"], "setup2")
    dma("sp", gF, gF_d, [], ["gF"], "setup2")
    wov = wout_d.rearrange("(c p) n -> p c n", p=128)
    for c2 in range(2):
        dma("pool", Wo[:, c2 * 4:(c2 + 1) * 4, :], wov[:, c2 * 4:(c2 + 1) * 4, :], [], [("wo", c2)], "wog")
    for hf in range(2):
        for kc in range(8):
            dma("pool", W1[:, kc, hf * 2048:(hf + 1) * 2048], w1_d[kc * 128:(kc + 1) * 128, hf * 2048:(hf + 1) * 2048],
                [], [("w1", kc, hf)], f"w1g{hf}")
    w2v = w2_d.rearrange("(f p) n -> p f n", p=128)
    for f4 in range(8):
        dma("pool", W2[:, f4 * 4:(f4 + 1) * 4, :], w2v[:, f4 * 4:(f4 + 1) * 4, :], [], [("w2", f4)], "w2g")

    bank_ctr = [0]

    def bank2():
        bank_ctr[0] += 1
        return 2 + (bank_ctr[0] % 6)

    for T in range(NT // TT):
        tok0 = T * TT
        tg, to = tok0 // 512, tok0 % 512
        dma("sp", mxT, mx_d[tg, :, :, to:to + TT], [("mx", tg, "a")] + [("mx", tg, hp) for hp in range(4)], ["mxT"], "mxT")
        for sub in range(NS):
            sl = slice(sub * 128, (sub + 1) * 128)
            j = xs_ctr[0] % 3
            xs_ctr[0] += 1
            t0 = tok0 + sub * 128
            dma("sp", xs[j], x_d[NW + t0:NW + t0 + 128, :], [], [("xs", j)], f"xs{j}")
            for half in range(2):
                hs = slice(half * 512, (half + 1) * 512)
                b = bank2()
                for c in range(8):
                    S.op("pe", I("matmul", psf(b), mxT[:, c, sl], Wo[:, c, hs], start=(c == 0), stop=(c == 7)),
                         r=["mxT", ("wo", c // 4)], w=[f"ps{b}"])
                S.op("dve", I("tensor_tensor", hT[sub][:, hs], psf(b), xs[j][:, hs], ALU.add),
                     r=[f"ps{b}", ("xs", j)], w=[("hT", sub)])
        for sub in range(NS):
            rms_to_T(hT[sub], [("hT", sub)], xn2[sub % 2], ("xn2", sub % 2), hnT, ("hnT", sub), sub, 16 + 2 * (sub % 2),
                     gbc=gM, bank=sub % 2)
        for f in range(32):
            b = bank2()
            for kc in range(8):
                S.op("pe", I("matmul", psf(b)[:, 0:TT], W1[:, kc, f * 128:(f + 1) * 128], hnT[:, kc, :],
                             start=(kc == 0), stop=(kc == 7)), r=[("hnT", 0), ("hnT", 1), ("w1", kc, f // 16)], w=[f"ps{b}"])
            rj = f % 2
            S.op("act", I("activation", rsb[rj], psf(b)[:, 0:TT], AF.Relu), r=[f"ps{b}"], w=[("rsb", rj)])
            S.op("pool" if f % 2 else "dve", I("tensor_tensor", aT[:, f, :], rsb[rj], rsb[rj], ALU.mult),
                 r=[("rsb", rj)], w=[("aT", f)])
        for sub in range(NS):
            sl = slice(sub * 128, (sub + 1) * 128)
            for half in range(2):
                hs = slice(half * 512, (half + 1) * 512)
                b = bank2()
                for f in range(32):
                    S.op("pe", I("matmul", psf(b), aT[:, f, sl], W2[:, f, hs], start=(f == 0), stop=(f == 31)),
                         r=[("aT", f), ("w2", f // 4)], w=[f"ps{b}"])
                S.op("dve", I("tensor_tensor", hT[sub][:, hs], psf(b), hT[sub][:, hs], ALU.add),
                     r=[f"ps{b}", ("hT", sub)], w=[("hT", sub)])
            ssq = stat[:, 24:25]
            rstd = stat[:, 25:26]
            S.op("act", I("activation", xn2[0], hT[sub], AF.Square, accum_out=ssq),
                 r=[("hT", sub)], w=[("xn2", 0), ("stat", 24)])
            S.op("act", I("activation", rstd, ssq, AF.Ln, scale=1.0 / D, bias=epsT[:, 0:1]), r=[("stat", 24)], w=[("stat", 25)])
            S.op("act", I("activation", rstd, rstd, AF.Exp, scale=-0.5), r=[("stat", 25)], w=[("stat", 25)])
            S.op("dve", I("scalar_tensor_tensor", hT[sub], hT[sub], rstd, gF, ALU.mult, ALU.mult),
                 r=[("hT", sub), ("stat", 25), "gF"], w=[("hT", sub)])
            t0 = tok0 + sub * 128
            dma("pool", out_d[t0:t0 + 128, :], hT[sub], [("hT", sub)], [("out", t0)], f"hT{sub}")
    S.barrier()
    S.emit(nc, es)
    es.close()
    return nc, S


def _t5_bucket(dist):
    max_exact = 16
    n = np.maximum(dist, 0)
    large = max_exact + (np.log(np.maximum(n, 1) / max_exact) / math.log(2048 / max_exact) * (32 - max_exact)).astype(np.int32)
    large = np.minimum(large, 31)
    return np.where(n < max_exact, n, large).astype(np.int32)


def _consts():
    bf = ml_dtypes.bfloat16
    i = np.arange(128)
    c = {}
    c["ident"] = np.eye(128, dtype=np.float32).astype(bf)
    c["jflip"] = np.ascontiguousarray(np.eye(128, dtype=np.float32)[::-1])
    c["lincl"] = (i[:, None] <= i[None, :]).astype(np.float32)
    c["lst"] = (i[:, None] > i[None, :]).astype(np.float32)
    c["maskT"] = (i[:, None] <= i[None, :]).astype(np.float32).astype(bf)
    swp = np.zeros((128, 2, 128), np.float32)
    for m in range(64):
        swp[m + 64, 0, m] = 1.0
        swp[m, 1, m + 64] = 1.0
    c["swp"] = swp.reshape(128, 256)
    oh = np.zeros((33, 3, 383), np.float32)
    for br, d in enumerate(BRANCH_D):
        for j in range(383):
            s = j - 127
            if 0 <= s <= 128:
                oh[_t5_bucket(np.array([s * d]))[0], br, j] = 1.0
            else:
                oh[32, br, j] = NEG
    c["oh"] = oh.reshape(33, 3 * 383)
    return c


_CACHE = {}
N_CORES = 8


def _get_prog(NT, NW):
    if (NT, NW) not in _CACHE:
        _CACHE[(NT, NW)] = build(NT, NW)[0]
    return _CACHE[(NT, NW)]


def make_in_maps(inputs, core_specs):
    f = lambda a: np.ascontiguousarray(np.asarray(a, dtype=np.float32))
    x = f(inputs["x"])
    c = _consts()
    shared = dict(c)
    shared["w_in"] = f(inputs["w_in"][0])
    shared["gAb"] = np.ascontiguousarray(np.broadcast_to(f(inputs["attn_norm_g"][0])[None, :], (128, D)))
    shared["w2e"] = np.ascontiguousarray(np.concatenate([f(inputs["gla_gate_w2"][0]), f(inputs["gla_gate_b"][0])[None, :]], 0))
    shared["gng"] = np.ascontiguousarray(np.broadcast_to(f(inputs["gla_norm_g"][0])[None, :], (128, 512)))
    shared["relb"] = np.ascontiguousarray(np.concatenate([f(inputs["rel_bias"]), np.ones((1, 8), np.float32)], 0))
    shared["w_out"] = f(inputs["w_out"][0])
    shared["gMb"] = np.ascontiguousarray(np.broadcast_to(f(inputs["mlp_norm_g"][0])[None, :], (128, D)))
    shared["w_ff1"] = f(inputs["w_ff1"][0])
    shared["w_ff2"] = f(inputs["w_ff2"][0])
    shared["gF"] = np.ascontiguousarray(np.broadcast_to(f(inputs["final_norm_g"])[None, :], (128, D)))
    maps = []
    for (b, t0, nt, nw) in core_specs:
        m = dict(shared)
        xc = np.zeros((nw + nt, D), np.float32)
        lo = max(0, t0 - nw)
        xc[nw - (t0 - lo):] = x[b, lo:t0 + nt]
        m["x"] = xc
        m["halo"] = np.full((128, 1), NEG if t0 - REG < 0 else 0.0, np.float32)
        maps.append(m)
    return maps


def kernel(**inputs):
    x = np.asarray(inputs["x"])
    B, SEQ, _ = x.shape
    per = N_CORES // B
    NT = SEQ // per
    NW = NT if per > 1 else 0
    nc = _get_prog(NT, NW)
    specs = [(c // per, (c % per) * NT, NT, NW) for c in range(N_CORES)]
    in_maps = make_in_maps(inputs, specs)
    res = run_bass_kernel_spmd(nc, in_maps, core_ids=list(range(N_CORES)))
    out = np.empty((B, SEQ, D), np.float32)
    for c, (b, t0, nt, nw) in enumerate(specs):
        out[b, t0:t0 + nt] = np.asarray(res.results[c]["out"], dtype=np.float32)
    return out
```

```python
import math
from contextlib import ExitStack

import numpy as np
import ml_dtypes

import concourse.bass as bass
import concourse.mybir as mybir
from concourse.bass_utils import run_bass_kernel_spmd

F32 = mybir.dt.float32
BF = mybir.dt.bfloat16
ALU = mybir.AluOpType
AF = mybir.ActivationFunctionType

D = 1024
DIN = 3088
DFF = 4096
C_GQ, C_GK, C_GV, C_GR, C_GL, C_DQ, C_DK, C_DV = 0, 256, 512, 1024, 1536, 1552, 2064, 2576
EPS = 1e-6
NEG = -1e30
REG = 2048
RING = 4096
BRANCH_D = (1, 4, 16)


class Sched:
    ENGS = ("pe", "act", "dve", "pool", "sp")

    def __init__(self):
        self.ops = []
        self.lastw = {}
        self.readers = {}
        self.last_real = {}
        self.last_dma = {}
        self.group_total = {"setup"}

    max_ops = None

    def op(self, eng, fn, r=(), w=(), dma=None, extra=()):
        i = len(self.ops)
        if self.max_ops is not None and i >= self.max_ops and fn is not None:
            return None
        w = list(w) + [k for k in r if isinstance(k, str) and k[:2] == "ps" and k[2:3].isdigit()]
        deps = set(extra)
        for k in r:
            j = self.lastw.get(k)
            if j is not None:
                deps.add(j)
        for k in w:
            j = self.lastw.get(k)
            if j is not None:
                deps.add(j)
            for j in self.readers.get(k, ()):
                deps.add(j)
        deps.discard(i)
        for k in w:
            self.lastw[k] = i
            self.readers[k] = []
        for k in r:
            lst = self.readers.setdefault(k, [])
            if dma is None:
                lst[:] = [j for j in lst if not (self.ops[j]["dma"] is None and self.ops[j]["eng"] == eng)]
            lst.append(i)
        self.ops.append(dict(eng=eng, fn=fn, deps=deps, dma=dma, sig=None))
        if fn is not None:
            if dma is not None:
                self.last_dma[dma] = i
            else:
                self.last_real[eng] = i
        return i

    def barrier(self):
        extra = set(self.last_real.values()) | set(self.last_dma.values())
        for e in self.ENGS:
            self.op(e, None, extra=extra)

    def emit(self, nc, es):
        ops = self.ops
        needed = [False] * len(ops)
        for o in ops:
            for j in o["deps"]:
                pj = ops[j]
                if pj["dma"] is None and o["dma"] is None and pj["eng"] == "pe" and o["eng"] == "pe" \
                        and o["fn"] is not None:
                    continue
                needed[j] = True
        cnt = {}
        for i, o in enumerate(ops):
            if o["fn"] is None:
                continue
            if o["dma"] is not None:
                key = "d_" + o["dma"]
                cnt[key] = cnt.get(key, 0) + 16
                o["sig"] = (key, cnt[key])
            elif needed[i]:
                key = "e_" + o["eng"]
                cnt[key] = cnt.get(key, 0) + 1
                o["sig"] = (key, cnt[key])
        sems = {k: es.enter_context(nc.semaphore(k)) for k in cnt}
        self.sem_counts = cnt
        per_eng = {e: [o for o in ops if o["eng"] == e] for e in self.ENGS}

        def run(engname, e):
            waited = {}
            for o in per_eng[engname]:
                want = {}
                for j in o["deps"]:
                    pj = ops[j]
                    if pj["sig"] is None:
                        continue
                    if pj["dma"] is None and o["dma"] is None and pj["eng"] == "pe" and engname == "pe" \
                            and o["fn"] is not None:
                        continue
                    key, val = pj["sig"]
                    if pj["dma"] in self.group_total:
                        val = cnt[key]
                    if val > want.get(key, 0):
                        want[key] = val
                for key, val in want.items():
                    if waited.get(key, 0) >= val:
                        continue
                    e.wait_ge(sems[key], val)
                    waited[key] = val
                if o["fn"] is not None:
                    ins = o["fn"](e)
                    if o["sig"] is not None:
                        ins.then_inc(sems[o["sig"][0]], 16 if o["dma"] is not None else 1)

        with nc.Block() as block:
            @block.tensor
            def _(e):
                run("pe", e)

            @block.scalar
            def _(e):
                run("act", e)

            @block.vector
            def _(e):
                run("dve", e)

            @block.gpsimd
            def _(e):
                run("pool", e)

            @block.sync
            def _(e):
                run("sp", e)


def I(name, *a, **k):
    return lambda e: getattr(e, name)(*a, **k)


class Carver:
    def __init__(self, big, start, limit):
        self.big, self.off, self.limit = big, start, limit

    def get(self, dtype, free, parts=128):
        n = int(np.prod(free))
        nb = n * (4 if dtype is F32 else 2)
        off = self.off
        self.off += (nb + 63) // 64 * 64
        assert self.off <= self.limit, (self.off, self.limit)
        ap = self.big[0:parts, off // 2: off // 2 + nb // 2]
        if dtype is F32:
            ap = ap.bitcast(F32)
        if len(free) == 2:
            ap = ap.rearrange("p (a b) -> p a b", a=free[0])
        elif len(free) == 3:
            ap = ap.rearrange("p (a b c) -> p a b c", a=free[0], b=free[1])
        elif len(free) == 4:
            ap = ap.rearrange("p (a b c d) -> p a b c d", a=free[0], b=free[1], c=free[2])
        return ap


def build(NT, NW=0, stop_after=None):
    first_region_has_prev = False
    NREG = (NT + NW) // REG
    NWR = NW // REG
    nc = bass.Bass("TRN2", target_bir_lowering=False)
    es = ExitStack()
    S = Sched()
    if isinstance(stop_after, int):
        S.max_ops = stop_after

    def din(name, shape, dt=F32):
        return nc.dram_tensor(name, list(shape), dt, kind="ExternalInput").ap()

    x_d = din("x", [NT + NW, D])
    halo_d = din("halo", [128, 1])
    win_d = din("w_in", [D, DIN])
    gA_d = din("gAb", [128, D])
    w2e_d = din("w2e", [17, 256])
    gng_d = din("gng", [128, 512])
    relb_d = din("relb", [33, 8])
    oh_d = din("oh", [33, 3 * 383])
    wout_d = din("w_out", [D, D])
    gMb_d = din("gMb", [128, D])
    w1_d = din("w_ff1", [D, DFF])
    w2_d = din("w_ff2", [DFF, D])
    gF_d = din("gF", [128, D])
    ident_d = din("ident", [128, 128], BF)
    jflip_d = din("jflip", [128, 128])
    lincl_d = din("lincl", [128, 128])
    lst_d = din("lst", [128, 128])
    maskT_d = din("maskT", [128, 128], BF)
    swp_d = din("swp", [128, 256])
    out_d = nc.dram_tensor("out", [NT, D], F32, kind="ExternalOutput").ap()
    gx_t = nc.dram_tensor("gx_scr", [8, 3 * 383], F32, kind="Internal")
    gx_d = gx_t.ap()
    bias_d = nc.dram_tensor("bias_scr", [128, 3 * 8 * 256], BF, kind="Internal").ap()
    mx_d = nc.dram_tensor("mx_scr", [NT // 512, 128, 8, 512], BF, kind="Internal").ap()

    NBF = 106000
    big = es.enter_context(nc.sbuf_tensor("big", [128, NBF], BF))
    ps = [es.enter_context(nc.psum_tensor(f"ps{i}", [128, 512], F32)) for i in range(8)]

    def psf(i):
        return ps[i][:, :]

    def psb(i):
        return ps[i][:, :].bitcast(BF)

    CM = Carver(big, 0, NBF * 2)
    ident = CM.get(BF, [128])
    xs = [CM.get(F32, [1024]) for _ in range(3)]
    sq_junk = CM.get(BF, [128])
    epsT = CM.get(F32, [1])
    stat = CM.get(F32, [64])
    COMMON_END = CM.off

    def dma(q, out, in_, r, w, key):
        return S.op(q, I("dma_start", out=out, in_=in_), r=r, w=w, dma=key)

    def cp(eng, out, in_, r, w):
        S.op(eng, I("copy" if eng == "act" else "tensor_copy", out, in_), r=r, w=w)

    dma("sp", ident, ident_d, [], ["ident"], "setup")
    S.op("pool", I("memset", epsT, EPS), w=["epsT"])

    P1 = Carver(big, COMMON_END, NBF * 2)
    Win = P1.get(BF, [8, DIN])
    gA = P1.get(F32, [D])
    w2e = P1.get(F32, [256], parts=17)
    gng = P1.get(F32, [512])
    lincl = P1.get(F32, [128])
    lst = P1.get(F32, [128])
    maskT = P1.get(BF, [128])
    swp = P1.get(F32, [2, 128])
    negM = P1.get(F32, [8])
    haloM = P1.get(F32, [1])
    qT = P1.get(BF, [4, REG])
    kT = P1.get(BF, [4, RING])
    vT = P1.get(BF, [4, RING])
    Sf = P1.get(F32, [2, 128])
    Sb = P1.get(BF, [2, 128])
    STAGE0 = P1.off
    PA = Carver(big, STAGE0, NBF * 2)
    xn = [PA.get(BF, [1024]) for _ in range(2)]
    xnT = PA.get(BF, [8, 512])
    mixA = [PA.get(BF, [4, 512]) for _ in range(1)]
    gqT_sb = PA.get(BF, [2, 512])
    gkT_sb = PA.get(BF, [2, 512])
    glow = [PA.get(F32, [512], parts=17) for _ in range(1)]
    G = []
    for _ in range(2):
        G.append(dict(
            gk_sb=PA.get(F32, [256]), gr_sb=PA.get(F32, [512]), v_bf=PA.get(BF, [512]),
            lsp=PA.get(F32, [256]), EbT=PA.get(F32, [2, 128]), EnbT=PA.get(F32, [2, 128]), Erem=PA.get(F32, [256]),
            qdp=PA.get(BF, [2, 2, 128]), kiT=PA.get(BF, [2, 128]), kend=PA.get(BF, [256]), attm=PA.get(BF, [4, 128]),
            ep_e=PA.get(F32, [512]), ep_on=PA.get(F32, [512]), mix_tm=PA.get(BF, [512])))
    PB = Carver(big, STAGE0, NBF * 2)
    ACC = PB.get(F32, [2, REG])
    mixD = PB.get(BF, [REG])
    PT = [PB.get(BF, [2, 2, 128]) for _ in range(4)]
    Vb = [PB.get(BF, [2, 128]) for _ in range(8)]
    qpad = PB.get(BF, [2, REG])
    recb = PB.get(F32, [512])
    biasT = PB.get(BF, [3, 8, 256])
    P0 = Carver(big, STAGE0, NBF * 2)
    Hst = [P0.get(F32, [256]) for _ in range(4)]
    relb = P0.get(F32, [8], parts=33)
    oh = P0.get(F32, [3 * 383], parts=33)
    gxs = P0.get(F32, [3 * 383], parts=8)
    jflip = P0.get(F32, [128])
    print("SBUF bytes: common", COMMON_END, "persist", STAGE0, "A", PA.off, "B", PB.off, "setup", P0.off)

    for dst, src, k in ((haloM, halo_d, "haloM"), (gA, gA_d, "gA"), (w2e, w2e_d, "w2e"), (gng, gng_d, "gng"), (jflip, jflip_d, "jflip"),
                        (lincl, lincl_d, "lincl"), (lst, lst_d, "lst"), (maskT, maskT_d, "maskT"),
                        (swp, swp_d.rearrange("p (a b) -> p a b", a=2), "swp"), (relb, relb_d, "relb"),
                        (oh, oh_d, "oh")):
        dma("sp", dst, src, [], [k], "setup")
    S.op("pool", I("memset", negM, 0.0), w=["negM"])
    S.op("pool", I("memset", Sf, 0.0), w=["Sf"])
    S.op("pool", I("memset", Sb, 0.0), w=["Sb"])

    cast_engs = ("dve", "pool", "act")
    S.group_total.add("setupw")
    for kc in range(8):
        dma("pool", Win[:, kc, :], win_d[kc * 128:(kc + 1) * 128, :], [], [("win", kc)], "setupw")

    for br in range(3):
        S.op("pe", I("matmul", psf(4)[0:8, 0:383], relb, oh[:, br * 383:(br + 1) * 383], start=True, stop=True),
             r=["relb", "oh"], w=["ps4"])
        cp("dve", gxs[:, br * 383:(br + 1) * 383], psf(4)[0:8, 0:383], ["ps4"], ["gxs"])
    dma("sp", gx_d, gxs, ["gxs"], ["gx_d"], "gx")
    bi = 0
    for br in range(3):
        for h in range(8):
            j = bi % 4
            for kh in range(2):
                src = bass.AP(gx_t, h * (3 * 383) + br * 383 + 128 * (1 - kh), [[1, 128], [1, 128]])
                dma("sp", Hst[j][:, kh * 128:(kh + 1) * 128], src, ["gx_d"], [("Hst", j, kh)], f"Hst{j}{kh}")
            pb = 4 + (bi % 2)
            S.op("pe", I("matmul", psf(pb)[:, 0:256], jflip, Hst[j], start=True, stop=True),
                 r=[("Hst", j, 0), ("Hst", j, 1), "jflip"], w=[f"ps{pb}"])
            cp("act" if bi % 2 else "dve", biasT[:, br, h, :], psf(pb)[:, 0:256], [f"ps{pb}"], ["biasT"])
            bi += 1
    dma("sp", bias_d, biasT.rearrange("p a b c -> p (a b c)"), ["biasT"], ["bias_d"], "biasd")
    S.barrier()
    if stop_after == "setup":
        S.emit(nc, es)
        es.close()
        return nc, S, dict(biasT=biasT, Win=Win)

    evac_ctr = [0]

    def evac(out, in_, r, w, scale=None):
        evac_ctr[0] += 1
        if evac_ctr[0] % 2:
            if scale is None:
                S.op("act", I("copy", out, in_), r=r, w=w)
            else:
                S.op("act", I("activation", out, in_, AF.Copy, scale=scale), r=r, w=w)
        else:
            if scale is None:
                S.op("dve", I("tensor_copy", out, in_), r=r, w=w)
            else:
                S.op("dve", I("tensor_scalar", out, in_, scale, None, ALU.mult), r=r, w=w)

    mm_ctr = [0]

    def mm_bank():
        mm_ctr[0] += 1
        return 2 + (mm_ctr[0] % 2)

    xs_ctr = [0]

    def rms_to_T(src, src_keys, xn_t, xn_key, dstT, dst_key, sub, st_off, gbc=None, bank=0, defer=False):
        ssq = stat[:, st_off:st_off + 1]
        rstd = stat[:, st_off + 1:st_off + 2]
        bk = f"ps{bank}"
        S.op("act", I("activation", xn_t, src, AF.Square, accum_out=ssq),
             r=src_keys, w=[xn_key, ("stat", st_off)])
        S.op("act", I("activation", rstd, ssq, AF.Ln, scale=1.0 / D, bias=epsT[:, 0:1]),
             r=[("stat", st_off)], w=[("stat", st_off + 1)])
        S.op("act", I("activation", rstd, rstd, AF.Exp, scale=-0.5), r=[("stat", st_off + 1)], w=[("stat", st_off + 1)])
        if gbc is None:
            S.op("act", I("activation", xn_t, src, AF.Copy, scale=rstd),
                 r=src_keys + [("stat", st_off + 1)], w=[xn_key])
        else:
            S.op("dve", I("scalar_tensor_tensor", xn_t, src, rstd, gbc, ALU.mult, ALU.mult),
                 r=src_keys + [("stat", st_off + 1), "gM"], w=[xn_key])
        def part_b():
            for kc in range(8):
                S.op("pe", I("transpose", psb(bank)[:, kc * 128:(kc + 1) * 128], xn_t[:, kc * 128:(kc + 1) * 128], ident),
                     r=[xn_key], w=[bk])
            S.op("dve", I("tensor_copy", dstT[:, :, sub * 128:(sub + 1) * 128], psb(bank).rearrange("p (a b) -> p a b", a=8)),
                 r=[bk], w=[dst_key])
        if defer:
            return part_b
        part_b()

    def projA(c0, M, out, wkey, scale=None):
        b = mm_bank()
        for kc in range(8):
            S.op("pe", I("matmul", psf(b)[0:M, :], Win[:, kc, c0:c0 + M], xnT[:, kc, :], start=(kc == 0), stop=(kc == 7)),
                 r=[("xnT", 0), ("xnT", 1), ("xnT", 2), ("xnT", 3)], w=[f"ps{b}"])
        evac(out, psf(b)[0:M, :], [f"ps{b}"], [wkey], scale=scale)

    def projB(sub, c0, N, b):
        for kc in range(8):
            S.op("pe", I("matmul", psf(b)[:, 0:N], xnT[:, kc, sub * 128:(sub + 1) * 128], Win[:, kc, c0:c0 + N],
                         start=(kc == 0), stop=(kc == 7)),
                 r=[("xnT", sub)], w=[f"ps{b}"])

    P2 = Carver(big, COMMON_END, NBF * 2)
    W1 = P2.get(BF, [8, DFF])
    W2 = P2.get(BF, [32, D])
    Wo = P2.get(BF, [8, D])
    gM = P2.get(F32, [D])
    gF = P2.get(F32, [D])
    TT = 256
    NS = TT // 128
    mxT = P2.get(BF, [8, TT])
    hT = [P2.get(F32, [D]) for _ in range(NS)]
    xn2 = [P2.get(BF, [D]) for _ in range(2)]
    hnT = P2.get(BF, [8, TT])
    aT = P2.get(BF, [32, TT])
    rsb = [P2.get(BF, [TT]) for _ in range(2)]
    print("SBUF bytes phase2:", P2.off)

    S.group_total.add("setup2p")
    S.group_total.add("w1early")
    for R in range(NREG):
        main = R >= NWR
        warm_kv = (not main) and R == NWR - 1
        for g in glow:
            S.op("pool", I("memset", g, 1.0), w=[("glow", id(g))])
        for gi in range(2):
            S.op("pool", I("memset", G[gi]["qdp"], 0.0), w=[("qdT", gi)])

        def rms_part(T, sub, defer=False):
            tok0 = R * REG + T * 512
            j = xs_ctr[0] % 3
            xs_ctr[0] += 1
            t0 = tok0 + sub * 128
            dma("sp", xs[j], x_d[t0:t0 + 128, :], [], [("xs", j)], f"xs{j}")
            return rms_to_T(xs[j], [("xs", j)], xn[sub % 2], ("xn", sub % 2), xnT, ("xnT", sub), sub, 2 * (sub % 2),
                            gbc=gA, bank=sub % 2, defer=defer)

        def proj_part(T):
            tok0 = R * REG + T * 512
            rt0 = T * 512
            ring0 = tok0 % RING
            if main:
                for c in range(2):
                    projA(C_GQ + c * 128, 128, gqT_sb[:, c, :], ("gqT", c))
                for c in range(2):
                    projA(C_GK + c * 128, 128, gkT_sb[:, c, :], ("gkT", c))
            gl = glow[0]
            projA(C_GL, 16, gl[0:16, :], ("glow", id(gl)))
            if main:
                for c in range(4):
                    projA(C_DQ + c * 128, 128, qT[:, c, rt0:rt0 + 512], "qT", scale=0.125)
            if main or warm_kv:
                for c in range(4):
                    projA(C_DK + c * 128, 128, kT[:, c, ring0:ring0 + 512], "kT")
                for c in range(4):
                    projA(C_DV + c * 128, 128, vT[:, c, ring0:ring0 + 512], "vT")

        def F1(si):
            T, sub = divmod(si, 4)
            g = G[si % 2]; p = si % 2
            sl = slice(sub * 128, (sub + 1) * 128)
            gl = glow[0]
            glk = ("glow", id(gl))
            S.op("pe", I("matmul", psf(6)[:, 0:256], gl[:, sl], w2e, start=True, stop=True), r=[glk], w=["ps6"])
            S.op("act", I("activation", g["lsp"], psf(6)[:, 0:256], AF.Exp, scale=-1.0), r=["ps6"], w=[("lsp", p)])
            S.op("act", I("activation", g["lsp"], g["lsp"], AF.Ln, bias=1.0), r=[("lsp", p)], w=[("lsp", p)])
            projB(sub, C_GK, 256, 2)
            cp("act", g["gk_sb"], psf(2)[:, 0:256], ["ps2"], [("gk_sb", p)])
            projB(sub, C_GV, 512, 3)
            cp("dve", g["v_bf"], psf(3), ["ps3"], [("v_bf", p)])
            if main:
                projB(sub, C_GR, 512, 2)
                S.op("act", I("activation", g["ep_e"], psf(2), AF.Exp, scale=-1.0), r=["ps2"], w=[("ep_e", p)])
                S.op("dve", I("tensor_tensor", g["gr_sb"], psf(2), gng, ALU.mult), r=["ps2"], w=[("gr_sb", p)])
                S.op("act", I("activation", g["ep_e"], g["ep_e"], AF.Ln, bias=1.0), r=[("ep_e", p)], w=[("ep_e", p)])
                S.op("act", I("activation", g["ep_e"], g["ep_e"], AF.Exp, scale=-1.0), r=[("ep_e", p)], w=[("ep_e", p)])
                S.op("pool", I("tensor_tensor", g["gr_sb"], g["gr_sb"], g["ep_e"], ALU.mult),
                     r=[("gr_sb", p), ("ep_e", p)], w=[("gr_sb", p)])

        def F2(si):
            T, sub = divmod(si, 4)
            g = G[si % 2]; p = si % 2
            cb = 4 if main else 5
            sl = slice(sub * 128, (sub + 1) * 128)
            lsp = g["lsp"]
            for c in range(2):
                S.op("pe", I("matmul", psf(cb)[:, c * 128:(c + 1) * 128], lsp[:, c * 128:(c + 1) * 128], lincl,
                             start=True, stop=True), r=[("lsp", p)], w=[f"ps{cb}"])
            S.op("pe", I("matmul", psf(cb)[:, 256:512], lst, lsp, start=True, stop=True), r=[("lsp", p)], w=[f"ps{cb}"])
            cT = psf(cb)[:, 0:256].rearrange("p (a b) -> p a b", a=2)
            if main:
                S.op("act", I("activation", g["EbT"], cT, AF.Exp, scale=-1.0 / 16), r=[f"ps{cb}"], w=[("EbT", p)])
                S.op("act", I("activation", g["EnbT"], cT, AF.Exp, scale=1.0 / 16), r=[f"ps{cb}"], w=[("EnbT", p)])
            else:
                S.op("act", I("activation", g["EbT"][:, :, 127:128], cT[:, :, 127:128], AF.Exp, scale=-1.0 / 16),
                     r=[f"ps{cb}"], w=[("EbT", p)])
            S.op("act", I("activation", g["Erem"], psf(cb)[:, 256:512], AF.Exp, scale=-1.0 / 16), r=[f"ps{cb}"], w=[("Erem", p)])
            if main:
                for hd in range(2):
                    pr = slice(hd * 64, (hd + 1) * 64)
                    S.op("dve", I("scalar_tensor_tensor", g["qdp"][pr, :, hd, :], gqT_sb[pr, :, sl], 0.125, g["EbT"][pr, :, :],
                                  ALU.mult, ALU.mult), r=[("gqT", 0), ("gqT", 1), ("EbT", p)], w=[("qdT", p)])
                S.op("pool", I("tensor_tensor", g["kiT"], gkT_sb[:, :, sl], g["EnbT"], ALU.mult),
                     r=[("gkT", 0), ("gkT", 1), ("EnbT", p)], w=[("kiT", p)])
            S.op("pool", I("tensor_tensor", g["kend"], g["gk_sb"], g["Erem"], ALU.mult),
                 r=[("gk_sb", p), ("Erem", p)], w=[("kend", p)])

        def B1(si):
            g = G[si % 2]; p = si % 2
            if not main:
                return
            for h in range(4):
                c = h // 2
                S.op("pe", I("matmul", psf(6)[:, h * 128:(h + 1) * 128], g["kiT"][:, c, :], g["qdp"][:, c, h % 2, :],
                             start=True, stop=True), r=[("kiT", p), ("qdT", p)], w=["ps6"])
            S.op("dve", I("tensor_tensor", g["attm"], psf(6).rearrange("p (a b) -> p a b", a=4),
                          maskT.unsqueeze(1).to_broadcast([128, 4, 128]), ALU.mult), r=["ps6"], w=[("attm", p)])

        def B2(si):
            g = G[si % 2]; p = si % 2
            ob = (7, 5)[si % 2]; obk = f"ps{ob}"
            v_bf, kend, EbT = g["v_bf"], g["kend"], g["EbT"]
            for h in (range(4) if main else ()):
                c = h // 2
                S.op("pe", I("matmul", psf(ob)[:, h * 128:(h + 1) * 128], g["attm"][:, h, :], v_bf[:, h * 128:(h + 1) * 128],
                             start=True, stop=False), r=[("attm", p), ("v_bf", p)], w=[obk])
                S.op("pe", I("matmul", psf(ob)[:, h * 128:(h + 1) * 128], g["qdp"][:, c, h % 2, :], Sb[:, c, :],
                             start=False, stop=True), r=[("qdT", p), "Sb"], w=[obk])
            for c in range(2):
                for hd in range(2):
                    h = 2 * c + hd
                    S.op("pe", I("matmul", psf(4)[:, c * 256 + hd * 128:c * 256 + (hd + 1) * 128], kend[:, c * 128:(c + 1) * 128],
                                 v_bf[:, h * 128:(h + 1) * 128], start=True, stop=True),
                         r=[("kend", p), ("v_bf", p)], w=["ps4"])
            for c in range(2):
                for hd in range(2):
                    pr = slice(hd * 64, (hd + 1) * 64)
                    S.op("dve", I("scalar_tensor_tensor", Sf[pr, c, :], Sf[pr, c, :], EbT[pr, c, 127:128],
                                  psf(4)[pr, c * 256 + hd * 128:c * 256 + (hd + 1) * 128], ALU.mult, ALU.add),
                         r=["ps4", ("EbT", p), "Sf"], w=["Sf"])
            cp("dve", Sb, Sf, ["Sf"], ["Sb"])

        def B3(si):
            T, sub = divmod(si, 4)
            g = G[si % 2]; p = si % 2
            ob = (7, 5)[si % 2]; obk = f"ps{ob}"
            sl = slice(sub * 128, (sub + 1) * 128)
            if not main:
                return
            ep_e, ep_on, gr_sb, mix_tm = g["ep_e"], g["ep_on"], g["gr_sb"], g["mix_tm"]
            o4 = psf(ob).rearrange("p (a b) -> p a b", a=4)
            for h in range(4):
                S.op("act", I("activation", sq_junk[:, 0:128], psf(ob)[:, h * 128:(h + 1) * 128], AF.Square,
                              accum_out=stat[:, 8 + h:9 + h]), r=[obk], w=["sq_junk", ("stat", 8)])
            S.op("dve", I("tensor_scalar", stat[:, 12:16], stat[:, 8:12], 1.0 / 128, EPS, ALU.mult, ALU.add),
                 r=[("stat", 8)], w=[("stat", 12)])
            S.op("act", I("activation", stat[:, 12:16], stat[:, 12:16], AF.Ln), r=[("stat", 12)], w=[("stat", 12)])
            S.op("act", I("activation", stat[:, 12:16], stat[:, 12:16], AF.Exp, scale=-0.5),
                 r=[("stat", 12)], w=[("stat", 12)])
            S.op("dve", I("tensor_tensor", ep_on.rearrange("p (a b) -> p a b", a=4), o4,
                          stat[:, 12:16].unsqueeze(2).to_broadcast([128, 4, 128]), ALU.mult),
                 r=[obk, ("stat", 12)], w=[("ep_on", p)])
            S.op("dve", I("tensor_tensor", mix_tm, ep_on, gr_sb, ALU.mult), r=[("ep_on", p), ("gr_sb", p)], w=[("mix_tm", p)])

        def B3b(si):
            T, sub = divmod(si, 4)
            g = G[si % 2]; p = si % 2
            sl = slice(sub * 128, (sub + 1) * 128)
            if not main:
                return
            mix_tm = g["mix_tm"]
            for c in range(4):
                S.op("pe", I("transpose", psb(1)[:, c * 128:(c + 1) * 128], mix_tm[:, c * 128:(c + 1) * 128], ident),
                     r=[("mix_tm", p)], w=["ps1"])
            cp("act", mixA[0][:, :, sl], psb(1)[:, 0:512].rearrange("p (a b) -> p a b", a=4), ["ps1"], [("mixA", 0)])
            if sub == 3:
                tg = (R * REG + T * 512 - NW) // 512
                dma("pool", mx_d[tg, :, 0:4, :], mixA[0], [("mixA", 0)], [("mx", tg, "a")], "mixA0")

        NSUB = REG // 128
        for sub in range(4):
            rms_part(0, sub)
        late = None
        for si in range(NSUB + 2):
            if si < NSUB:
                T, sub = divmod(si, 4)
                if sub == 0 and si >= 2:
                    B3b(si - 2)
                    B1(si - 1)
                if sub == 0:
                    proj_part(T)
                if T + 1 < NSUB // 4:
                    late = rms_part(T + 1, sub, defer=True)
                F1(si)
            if si >= 2 and not (si < NSUB and si % 4 == 0):
                B3b(si - 2)
            if 1 <= si <= NSUB and not (si < NSUB and si % 4 == 0 and si >= 2):
                B1(si - 1)
            if si < NSUB:
                F2(si)
            if 1 <= si <= NSUB:
                B2(si - 1)
                B3(si - 1)
            if late is not None:
                late()
                late = None
        if not main:
            continue
        S.barrier()
        if stop_after == "A":
            S.emit(nc, es)
            es.close()
            return nc, S, {}

        dma("sp", biasT.rearrange("p a b c -> p (a b c)"), bias_d, ["bias_d"], ["biasT"], "biasl")
        for v in Vb:
            S.op("pool", I("memset", v, 1.0), w=[("Vb", id(v))])
        S.op("pool", I("memset", qpad, 0.0), w=["qpad"])
        for hp in range(4):
            for q_ in range(2):
                qs_ = slice(q_ * 1024, (q_ + 1) * 1024)
                S.op("act", I("copy", qpad[0:64, 0, qs_], qT[0:64, hp, qs_]), r=["qT"], w=["qpad"])
                S.op("dve", I("tensor_copy", qpad[64:128, 1, qs_], qT[64:128, hp, qs_]), r=["qT"], w=["qpad"])
            vcache = {}
            vb_ctr = [0]

            def get_vb(d, kstart_ring, ident_key):
                if ident_key in vcache:
                    return vcache[ident_key]
                slot = vb_ctr[0] % len(Vb)
                vb_ctr[0] += 1
                for k_, v_ in list(vcache.items()):
                    if v_ == slot:
                        del vcache[k_]
                vbank = slot % 2
                S.op("pe", I("transpose", psb(vbank)[:, 0:128], vT[:, hp, kstart_ring:kstart_ring + 127 * d + 1:d], ident),
                     r=["vT"], w=[f"ps{vbank}"])
                vflat = Vb[slot].rearrange("p a (b c) -> p (a b) c", b=2)
                cp("dve", vflat[:, 0:4:3, :], psb(vbank)[:, 0:128].rearrange("p (a b) -> p a b", a=2),
                   [f"ps{vbank}"], [("Vb", id(Vb[slot]))])
                vcache[ident_key] = slot
                return slot

            tiles = []
            for br, d in enumerate(BRANCH_D):
                span = 128 * d
                for r_ in range(d):
                    for c in range(REG // span):
                        qrel = c * span + r_
                        qabs = R * REG + qrel
                        has_prev = (qabs - span) >= 0
                        tiles.append(dict(br=br, d=d, span=span, qrel=qrel, qabs=qabs, has_prev=has_prev,
                                          kc_ring=qabs % RING, kp_ring=(qabs - span) % RING,
                                          halves=([0] if has_prev else []) + [1], i=len(tiles),
                                          qs=slice(qrel, qrel + 127 * d + 1, d)))

            def VT(t):
                t["vslots"] = {}
                if t["has_prev"]:
                    t["vslots"][0] = get_vb(t["d"], t["kp_ring"], (t["br"], t["qabs"] - t["span"]))
                t["vslots"][1] = get_vb(t["d"], t["kc_ring"], (t["br"], t["qabs"]))

            def SC(t):
                i, d, br, halves, qs = t["i"], t["d"], t["br"], t["halves"], t["qs"]
                sb_ = 4 + (i % 4)
                pt = PT[i % 4]
                ptk = ("PT", i % 4)
                S4 = psf(sb_).rearrange("p (a b c) -> p a b c", a=2, b=2)
                for hd in range(2):
                    for kh in halves:
                        kst = t["kp_ring"] if kh == 0 else t["kc_ring"]
                        S.op("pe", I("matmul", S4[:, hd, kh, :], kT[:, hp, kst:kst + 127 * d + 1:d],
                                     qpad[:, hd, qs], start=True, stop=False),
                             r=["kT", "qpad"], w=[f"ps{sb_}"])
                        S.op("pe", I("matmul", S4[:, hd, kh, :], ident,
                                     biasT[:, br, 2 * hp + hd, kh * 128:(kh + 1) * 128], start=False, stop=True),
                             r=["biasT"], w=[f"ps{sb_}"])
                k0 = halves[0]
                in_halo = t["has_prev"] and (t["qabs"] - t["span"]) < NW
                nb_ = negM[:, hp:hp + 1]
                if in_halo:
                    S.op("act", I("activation", pt[:, :, 0, :], S4[:, :, 0, :], AF.Exp, bias=haloM[:, 0:1]),
                         r=[f"ps{sb_}"], w=[ptk])
                    S.op("act", I("activation", pt[:, :, 1, :], S4[:, :, 1, :], AF.Exp, bias=nb_),
                         r=[f"ps{sb_}", "negM"], w=[ptk])
                else:
                    S.op("act", I("activation", pt[:, :, k0:2, :], S4[:, :, k0:2, :], AF.Exp, bias=nb_),
                         r=[f"ps{sb_}", "negM"], w=[ptk])

            def PV(t):
                i, halves, vslots, qs = t["i"], t["halves"], t["vslots"], t["qs"]
                pt = PT[i % 4]
                ptk = ("PT", i % 4)
                ud = 2 + (i % 2)
                UD = psf(ud)[:, 0:256].rearrange("p (a b) -> p a b", a=2)
                for hd in range(2):
                    for n_, kh in enumerate(halves):
                        S.op("pe", I("matmul", UD[:, hd, :], Vb[vslots[kh]][:, hd, :], pt[:, hd, kh, :],
                                     start=(n_ == 0), stop=(n_ == len(halves) - 1)),
                             r=[ptk, ("Vb", id(Vb[vslots[kh]]))], w=[f"ps{ud}"])
                if t["br"] == 0:
                    S.op("dve", I("tensor_copy", ACC[:, :, qs], UD), r=[f"ps{ud}"], w=["ACC"])
                else:
                    S.op("dve", I("tensor_tensor", ACC[:, :, qs], ACC[:, :, qs], UD, ALU.add),
                         r=[f"ps{ud}", "ACC"], w=["ACC"])

            nt_ = len(tiles)
            for i in range(nt_ + 2):
                if i < nt_:
                    VT(tiles[i])
                if 1 <= i <= nt_:
                    SC(tiles[i - 1])
                if i >= 2:
                    PV(tiles[i - 2])
            for sl4 in range(4):
                ts_ = slice(sl4 * 512, (sl4 + 1) * 512)
                S.op("pe", I("matmul", psf(1), swp[:, 0, :], ACC[:, 0, ts_], start=True, stop=False), r=["ACC"], w=["ps1"])
                S.op("pe", I("matmul", psf(1), swp[:, 1, :], ACC[:, 1, ts_], start=False, stop=True), r=["ACC"], w=["ps1"])
                S.op("act", I("activation", recb, psf(1), AF.Ln), r=["ps1"], w=["recb"])
                S.op("act", I("activation", recb, recb, AF.Exp, scale=-1.0), r=["recb"], w=["recb"])
                for hd in range(2):
                    pr = slice(hd * 64, (hd + 1) * 64)
                    S.op("dve", I("tensor_tensor", mixD[pr, ts_], ACC[pr, hd, ts_], recb[pr, :], ALU.mult),
                         r=["ACC", "recb"], w=["mixD"])
            for T in range(4):
                tg = (R * REG - NW) // 512 + T
                dma("pool", mx_d[tg, :, 4 + hp, :], mixD[:, T * 512:(T + 1) * 512], ["mixD"], [("mx", tg, hp)], "mixD")
        S.barrier()

    if stop_after == "B":
        S.emit(nc, es)
        es.close()
        return nc, S, {}
    for g_ in ("setup2", "wog", "w1g", "w2g"):
        S.group_total.add(g_)
    dma("sp", gM, gMb_d, [], ["gM"], "setup2")
    dma("sp", gF, gF_d, [], ["gF"], "setup2")
    wov = wout_d.rearrange("(c p) n -> p c n", p=128)
    for c2 in range(2):
        dma("pool", Wo[:, c2 * 4:(c2 + 1) * 4, :], wov[:, c2 * 4:(c2 + 1) * 4, :], [], [("wo", c2)], "wog")
    for kc in range(8):
        dma("pool", W1[:, kc, :], w1_d[kc * 128:(kc + 1) * 128, :], [], [("w1", kc)], "w1g")
    w2v = w2_d.rearrange("(f p) n -> p f n", p=128)
    for f4 in range(8):
        dma("pool", W2[:, f4 * 4:(f4 + 1) * 4, :], w2v[:, f4 * 4:(f4 + 1) * 4, :], [], [("w2", f4)], "w2g")

    bank_ctr = [0]

    def bank2():
        bank_ctr[0] += 1
        return 2 + (bank_ctr[0] % 6)

    for T in range(NT // TT):
        tok0 = T * TT
        tg, to = tok0 // 512, tok0 % 512
        dma("sp", mxT, mx_d[tg, :, :, to:to + TT], [("mx", tg, "a")] + [("mx", tg, hp) for hp in range(4)], ["mxT"], "mxT")
        for sub in range(NS):
            sl = slice(sub * 128, (sub + 1) * 128)
            j = xs_ctr[0] % 3
            xs_ctr[0] += 1
            t0 = tok0 + sub * 128
            dma("sp", xs[j], x_d[NW + t0:NW + t0 + 128, :], [], [("xs", j)], f"xs{j}")
            for half in range(2):
                hs = slice(half * 512, (half + 1) * 512)
                b = bank2()
                for c in range(8):
                    S.op("pe", I("matmul", psf(b), mxT[:, c, sl], Wo[:, c, hs], start=(c == 0), stop=(c == 7)),
                         r=["mxT", ("wo", c // 4)], w=[f"ps{b}"])
                S.op("dve", I("tensor_tensor", hT[sub][:, hs], psf(b), xs[j][:, hs], ALU.add),
                     r=[f"ps{b}", ("xs", j)], w=[("hT", sub)])
        for sub in range(NS):
            rms_to_T(hT[sub], [("hT", sub)], xn2[sub % 2], ("xn2", sub % 2), hnT, ("hnT", sub), sub, 16 + 2 * (sub % 2),
                     gbc=gM, bank=sub % 2)
        for f in range(32):
            b = bank2()
            for kc in range(8):
                S.op("pe", I("matmul", psf(b)[:, 0:TT], W1[:, kc, f * 128:(f + 1) * 128], hnT[:, kc, :],
                             start=(kc == 0), stop=(kc == 7)), r=[("hnT", 0), ("hnT", 1), ("w1", kc)], w=[f"ps{b}"])
            rj = f % 2
            S.op("act", I("activation", rsb[rj], psf(b)[:, 0:TT], AF.Relu), r=[f"ps{b}"], w=[("rsb", rj)])
            S.op("pool" if f % 2 else "dve", I("tensor_tensor", aT[:, f, :], rsb[rj], rsb[rj], ALU.mult),
                 r=[("rsb", rj)], w=[("aT", f)])
        for sub in range(NS):
            sl = slice(sub * 128, (sub + 1) * 128)
            for half in range(2):
                hs = slice(half * 512, (half + 1) * 512)
                b = bank2()
                for f in range(32):
                    S.op("pe", I("matmul", psf(b), aT[:, f, sl], W2[:, f, hs], start=(f == 0), stop=(f == 31)),
                         r=[("aT", f), ("w2", f // 4)], w=[f"ps{b}"])
                S.op("dve", I("tensor_tensor", hT[sub][:, hs], psf(b), hT[sub][:, hs], ALU.add),
                     r=[f"ps{b}", ("hT", sub)], w=[("hT", sub)])
            ssq = stat[:, 24:25]
            rstd = stat[:, 25:26]
            S.op("act", I("activation", xn2[0], hT[sub], AF.Square, accum_out=ssq),
                 r=[("hT", sub)], w=[("xn2", 0), ("stat", 24)])
            S.op("act", I("activation", rstd, ssq, AF.Ln, scale=1.0 / D, bias=epsT[:, 0:1]), r=[("stat", 24)], w=[("stat", 25)])
            S.op("act", I("activation", rstd, rstd, AF.Exp, scale=-0.5), r=[("stat", 25)], w=[("stat", 25)])
            S.op("dve", I("scalar_tensor_tensor", hT[sub], hT[sub], rstd, gF, ALU.mult, ALU.mult),
                 r=[("hT", sub), ("stat", 25), "gF"], w=[("hT", sub)])
            t0 = tok0 + sub * 128
            dma("pool", out_d[t0:t0 + 128, :], hT[sub], [("hT", sub)], [("out", t0)], f"hT{sub}")
    S.barrier()
    S.emit(nc, es)
    es.close()
    return nc, S


def _t5_bucket(dist):
    max_exact = 16
    n = np.maximum(dist, 0)
    large = max_exact + (np.log(np.maximum(n, 1) / max_exact) / math.log(2048 / max_exact) * (32 - max_exact)).astype(np.int32)
    large = np.minimum(large, 31)
    return np.where(n < max_exact, n, large).astype(np.int32)


def _consts():
    bf = ml_dtypes.bfloat16
    i = np.arange(128)
    c = {}
    c["ident"] = np.eye(128, dtype=np.float32).astype(bf)
    c["jflip"] = np.ascontiguousarray(np.eye(128, dtype=np.float32)[::-1])
    c["lincl"] = (i[:, None] <= i[None, :]).astype(np.float32)
    c["lst"] = (i[:, None] > i[None, :]).astype(np.float32)
    c["maskT"] = (i[:, None] <= i[None, :]).astype(np.float32).astype(bf)
    swp = np.zeros((128, 2, 128), np.float32)
    for m in range(64):
        swp[m + 64, 0, m] = 1.0
        swp[m, 1, m + 64] = 1.0
    c["swp"] = swp.reshape(128, 256)
    oh = np.zeros((33, 3, 383), np.float32)
    for br, d in enumerate(BRANCH_D):
        for j in range(383):
            s = j - 127
            if 0 <= s <= 128:
                oh[_t5_bucket(np.array([s * d]))[0], br, j] = 1.0
            else:
                oh[32, br, j] = NEG
    c["oh"] = oh.reshape(33, 3 * 383)
    return c


_CACHE = {}
N_CORES = 8


def _get_prog(NT, NW):
    if (NT, NW) not in _CACHE:
        _CACHE[(NT, NW)] = build(NT, NW)[0]
    return _CACHE[(NT, NW)]


def make_in_maps(inputs, core_specs):
    f = lambda a: np.ascontiguousarray(np.asarray(a, dtype=np.float32))
    x = f(inputs["x"])
    c = _consts()
    shared = dict(c)
    shared["w_in"] = f(inputs["w_in"][0])
    shared["gAb"] = np.ascontiguousarray(np.broadcast_to(f(inputs["attn_norm_g"][0])[None, :], (128, D)))
    shared["w2e"] = np.ascontiguousarray(np.concatenate([f(inputs["gla_gate_w2"][0]), f(inputs["gla_gate_b"][0])[None, :]], 0))
    shared["gng"] = np.ascontiguousarray(np.broadcast_to(f(inputs["gla_norm_g"][0])[None, :], (128, 512)))
    shared["relb"] = np.ascontiguousarray(np.concatenate([f(inputs["rel_bias"]), np.ones((1, 8), np.float32)], 0))
    shared["w_out"] = f(inputs["w_out"][0])
    shared["gMb"] = np.ascontiguousarray(np.broadcast_to(f(inputs["mlp_norm_g"][0])[None, :], (128, D)))
    shared["w_ff1"] = f(inputs["w_ff1"][0])
    shared["w_ff2"] = f(inputs["w_ff2"][0])
    shared["gF"] = np.ascontiguousarray(np.broadcast_to(f(inputs["final_norm_g"])[None, :], (128, D)))
    maps = []
    for (b, t0, nt, nw) in core_specs:
        m = dict(shared)
        xc = np.zeros((nw + nt, D), np.float32)
        lo = max(0, t0 - nw)
        xc[nw - (t0 - lo):] = x[b, lo:t0 + nt]
        m["x"] = xc
        m["halo"] = np.full((128, 1), NEG if t0 - REG < 0 else 0.0, np.float32)
        maps.append(m)
    return maps


def kernel(**inputs):
    x = np.asarray(inputs["x"])
    B, SEQ, _ = x.shape
    per = N_CORES // B
    NT = SEQ // per
    NW = NT if per > 1 else 0
    nc = _get_prog(NT, NW)
    specs = [(c // per, (c % per) * NT, NT, NW) for c in range(N_CORES)]
    in_maps = make_in_maps(inputs, specs)
    res = run_bass_kernel_spmd(nc, in_maps, core_ids=list(range(N_CORES)))
    out = np.empty((B, SEQ, D), np.float32)
    for c, (b, t0, nt, nw) in enumerate(specs):
        out[b, t0:t0 + nt] = np.asarray(res.results[c]["out"], dtype=np.float32)
    return out
```
